# Optimizing a Trainium2 kernel written in Bass

```python
import jax, jax.numpy as jnp
from jax import lax
import numpy as np

D_MODEL = 2048
BATCH = 8
SEQ = 2048
DEPTH = 2

N_MIXERS = 2
EXPAND = 2
BRANCH_WIDTH = EXPAND * D_MODEL
SC_WIDTH = 3
LRU_CONV_WIDTH = 4
LRU_HEAD_DIM = 256
LRU_HEADS = BRANCH_WIDTH // LRU_HEAD_DIM
RGLRU_C = 8.0
N_CONV_LAYERS = (DEPTH + 1) // N_MIXERS
N_LRU_LAYERS = DEPTH // N_MIXERS
EPS = 1e-6

kernel_name = "hybrid_shortconv_rglru_adaln"


def rmsnorm(x, g):
    xf = x.astype(jnp.float32)
    y = xf * lax.rsqrt(jnp.mean(xf * xf, axis=-1, keepdims=True) + EPS)
    return (y * g.astype(jnp.float32)).astype(x.dtype)


def adaln(h, c, w, b):
    mod = jnp.einsum('bd,df->bf', jax.nn.silu(c), w) + b
    shift, scale, gate = jnp.split(mod, 3, axis=-1)
    h = h * (1.0 + scale[:, None, :]) + shift[:, None, :]
    return h, gate


def causal_depthwise_conv(u, w):
    width, e = w.shape
    rhs = w[:, None, :].astype(u.dtype)
    return lax.conv_general_dilated(
        u, rhs, window_strides=(1,), padding=[(width - 1, 0)],
        dimension_numbers=('NWC', 'WIO', 'NWC'), feature_group_count=e)


def short_conv_mixer(h, w_in, conv_w, w_out):
    proj = jnp.einsum('bsd,de->bse', h, w_in)
    b_gate, c_gate, v, g = jnp.split(proj, 4, axis=-1)
    u = causal_depthwise_conv(c_gate * v, conv_w)
    y = b_gate * u * jax.nn.silu(g)
    return jnp.einsum('bse,ed->bsd', y, w_out)


def _linear_recurrence_combine(left, right):
    a_l, b_l = left
    a_r, b_r = right
    return a_l * a_r, a_r * b_l + b_r


def rglru_mixer(h, w_in, conv_w, conv_b, w_a, b_a, w_x, b_x, lam, w_out):
    proj = jnp.einsum('bsd,de->bse', h, w_in)
    v, g = jnp.split(proj, 2, axis=-1)
    v = causal_depthwise_conv(v, conv_w) + conv_b
    bsz, seq, e = v.shape
    n_heads, head_dim = w_a.shape[0], w_a.shape[1]
    vh = v.reshape(bsz, seq, n_heads, head_dim)
    r = jax.nn.sigmoid(jnp.einsum('bshi,hij->bshj', vh, w_a) + b_a).reshape(bsz, seq, e)
    i = jax.nn.sigmoid(jnp.einsum('bshi,hij->bshj', vh, w_x) + b_x).reshape(bsz, seq, e)
    log_a = -RGLRU_C * r.astype(jnp.float32) * jax.nn.softplus(-lam.astype(jnp.float32))
    a = jnp.exp(log_a)
    norm_mult = jnp.sqrt(-jnp.expm1(2.0 * log_a))
    b = norm_mult * (i * v).astype(jnp.float32)
    _, hs = lax.associative_scan(_linear_recurrence_combine, (a, b), axis=1)
    y = hs.astype(h.dtype) * jax.nn.silu(g)
    return jnp.einsum('bse,ed->bsd', y, w_out)


def setup_inputs(seed: int = 0) -> dict:
    key = jax.random.key(seed)
    ks = jax.random.split(key, 24)
    D, E, H, Dh = D_MODEL, BRANCH_WIDTH, LRU_HEADS, LRU_HEAD_DIM
    nA, nB = N_CONV_LAYERS, N_LRU_LAYERS
    f32 = jnp.float32
    n = lambda k, shape, s: jax.random.normal(k, shape, f32) * s
    x = jax.random.normal(ks[0], (BATCH, SEQ, D), f32)
    c = jax.random.normal(ks[1], (BATCH, D), f32)
    norm_g = 1.0 + n(ks[2], (DEPTH, D), 0.02)
    ada_w = n(ks[3], (DEPTH, D, 3 * D), 0.5 * D ** -0.5)
    ada_b = n(ks[4], (DEPTH, 3 * D), 0.02)
    sc_w_in = n(ks[5], (nA, D, 4 * E), D ** -0.5)
    sc_conv_w = n(ks[6], (nA, SC_WIDTH, E), SC_WIDTH ** -0.5)
    sc_w_out = n(ks[7], (nA, E, D), E ** -0.5)
    lru_w_in = n(ks[8], (nB, D, 2 * E), D ** -0.5)
    lru_conv_w = n(ks[9], (nB, LRU_CONV_WIDTH, E), LRU_CONV_WIDTH ** -0.5)
    lru_conv_b = n(ks[10], (nB, E), 0.02)
    lru_w_a = n(ks[11], (nB, H, Dh, Dh), Dh ** -0.5)
    lru_b_a = n(ks[12], (nB, H, Dh), 0.02)
    lru_w_x = n(ks[13], (nB, H, Dh, Dh), Dh ** -0.5)
    lru_b_x = n(ks[14], (nB, H, Dh), 0.02)
    a_pow_c = jax.random.uniform(ks[15], (nB, E), f32, minval=0.9, maxval=0.999)
    a0 = a_pow_c ** (1.0 / RGLRU_C)
    lru_lambda = jnp.log(a0) - jnp.log1p(-a0)
    lru_w_out = n(ks[16], (nB, E, D), E ** -0.5)
    final_g = 1.0 + n(ks[17], (D,), 0.02)
    return {
        "x": x, "c": c, "norm_g": norm_g, "ada_w": ada_w, "ada_b": ada_b,
        "sc_w_in": sc_w_in, "sc_conv_w": sc_conv_w, "sc_w_out": sc_w_out,
        "lru_w_in": lru_w_in, "lru_conv_w": lru_conv_w, "lru_conv_b": lru_conv_b,
        "lru_w_a": lru_w_a, "lru_b_a": lru_b_a, "lru_w_x": lru_w_x, "lru_b_x": lru_b_x,
        "lru_lambda": lru_lambda, "lru_w_out": lru_w_out, "final_g": final_g,
    }


def reference(x, c, norm_g, ada_w, ada_b, sc_w_in, sc_conv_w, sc_w_out,
              lru_w_in, lru_conv_w, lru_conv_b, lru_w_a, lru_b_a, lru_w_x, lru_b_x,
              lru_lambda, lru_w_out, final_g):
    for layer in range(DEPTH):
        j = layer // N_MIXERS
        h = rmsnorm(x, norm_g[layer])
        h, gate = adaln(h, c, ada_w[layer], ada_b[layer])
        if layer % N_MIXERS == 0:
            y = short_conv_mixer(h, sc_w_in[j], sc_conv_w[j], sc_w_out[j])
        else:
            y = rglru_mixer(h, lru_w_in[j], lru_conv_w[j], lru_conv_b[j],
                            lru_w_a[j], lru_b_a[j], lru_w_x[j], lru_b_x[j],
                            lru_lambda[j], lru_w_out[j])
        x = x + gate[:, None, :] * y
    return rmsnorm(x, final_g)
```

```python
import numpy as np
from contextlib import ExitStack
import concourse.bass as bass
import concourse.mybir as mybir
from concourse.bass_utils import run_bass_kernel_spmd

F32 = mybir.dt.float32
BF16 = mybir.dt.bfloat16
AF = mybir.ActivationFunctionType
ALU = mybir.AluOpType

D = 2048
E = 4096
T = 2048
KD = D // 128
KE = E // 128
NT = 1024
NBLK = T // NT
TC = 512
NTC = NT // TC
G = 8
NG = KE // G
EPS = 1e-6

NA = 5
NB = 3
NGR = 2
NY = 10
NBIG = 10

A_ADA0, A_W0, A_W1 = 0, 32, 160
NTILE_A = 224
B_OUT0, B_GATE, B_OUT1, B_ADAL = 0, 64, 80, 144
NTILE_B = 272

P_G0, P_G1, P_FG, P_AB0, P_AB1, P_SCW, P_LCW, P_LCB, P_BA, P_BX, P_LAM, P_C = (
    0, 16, 32, 48, 96, 144, 240, 368, 400, 432, 464, 496)
NPAR = 512
D_ASC0, D_ASC1, D_CN, D_HC, D_HBA, D_HBX, D_MOD0, D_MOD1 = 0, 16, 32, 64, 96, 128, 160, 208
NDER = 264
D_EPS, D_QTR = 256, 257


class Buf:
    __slots__ = ("name", "w", "rs", "rdma")

    def __init__(self, name):
        self.name = name
        self.w = None
        self.rs = {}
        self.rdma = []


class Op:
    __slots__ = ("eng", "fn", "deps", "sem", "inc", "idx", "val", "need", "is_dma")


class Sched:
    ENGS = ("tensor", "scalar", "vector", "gpsimd", "sync")

    def __init__(self, dry=False):
        self.q = {e: [] for e in self.ENGS}
        self.dma_count = {}
        self.dry = dry

    def op(self, eng, fn, reads=(), writes=(), dma_sem=None, extra=()):
        if self.dry:
            return None
        o = Op()
        o.eng = eng
        o.fn = fn
        o.is_dma = dma_sem is not None
        o.sem = dma_sem
        o.inc = 16 if o.is_dma else 1
        o.need = o.is_dma
        o.val = None
        deps = []
        for b in reads:
            if b.w is not None:
                deps.append(b.w)
        for b in writes:
            if b.w is not None:
                deps.append(b.w)
            deps.extend(b.rs.values())
            deps.extend(b.rdma)
        deps.extend(extra)
        o.deps = deps
        o.idx = len(self.q[eng])
        self.q[eng].append(o)
        for b in reads:
            if o.is_dma:
                b.rdma.append(o)
            else:
                b.rs[eng] = o
        for b in writes:
            b.w = o
            b.rs = {}
            b.rdma = []
        return o

    def finalize(self, eng_sems):
        for e in self.ENGS:
            for o in self.q[e]:
                for d in o.deps:
                    if d.is_dma:
                        continue
                    if d.eng == o.eng and d.eng == "tensor" and not o.is_dma:
                        continue
                    if d.eng != o.eng or o.is_dma or (o.idx - d.idx <= 2):
                        d.need = True
        for e in self.ENGS:
            cnt = 0
            for o in self.q[e]:
                if o.is_dma:
                    c = self.dma_count.get(id(o.sem), 0) + 16
                    self.dma_count[id(o.sem)] = c
                    o.val = c
                elif o.need:
                    cnt += 1
                    o.val = cnt
                    o.sem = eng_sems[e]

    def emit(self, eng_name, eng):
        waited = {}
        for o in self.q[eng_name]:
            ws = {}
            for d in o.deps:
                if not d.need:
                    continue
                if (not d.is_dma) and d.eng == o.eng and not o.is_dma and (
                        (o.idx - d.idx > 2) or d.eng == "tensor"):
                    continue
                k = id(d.sem)
                if k not in ws or ws[k][1] < d.val:
                    ws[k] = (d.sem, d.val)
            for k, (sem, val) in ws.items():
                if waited.get(k, 0) >= val:
                    continue
                eng.wait_ge(sem, val)
                waited[k] = val
            ins = o.fn(eng)
            if o.need:
                ins.then_inc(o.sem, o.inc)


class Ring:
    def __init__(self, n, seq):
        self.n = n
        self.seq = seq
        self.rec = []
        self.issued = 0
        self.consumed = 0
        self.rel = set()
        self.issue_fn = None

    def pump(self):
        if self.seq is None:
            return
        while self.issued < len(self.seq) and (self.issued < self.n or (self.issued - self.n) in self.rel):
            k = self.issued
            self.issue_fn(k, k % self.n, self.seq[k])
            self.issued += 1

    def acquire(self, tile):
        if self.seq is None:
            self.rec.append(tile)
            return len(self.rec) - 1
        k = self.consumed
        self.consumed += 1
        assert self.seq[k] == tile, (k, self.seq[k], tile)
        self.pump()
        assert self.issued > k, "ring deadlock: tile needed before its slot was released"
        return k

    def release(self, k):
        if self.seq is None:
            return
        self.rel.add(k)
        self.pump()


def build_program(layers=(0, 1), final_norm=True, nblk=NBLK):
    nc = bass.Bass("TRN2", target_bir_lowering=False)
    xT = nc.dram_tensor("xT", [D, T], F32, kind="ExternalInput").ap()
    tA = nc.dram_tensor("tA", [NTILE_A, 128, 2048], F32, kind="ExternalInput").ap()
    tB = nc.dram_tensor("tB", [NTILE_B, 128, 1024], F32, kind="ExternalInput").ap()
    par_d = nc.dram_tensor("par", [128, NPAR], F32, kind="ExternalInput").ap()
    outT = nc.dram_tensor("outT", [D, T], F32, kind="ExternalOutput").ap()

    with ExitStack() as es:
        def sb(name, shape, dt):
            return es.enter_context(nc.sbuf_tensor(name, shape, dt))

        def sem(name):
            return es.enter_context(nc.semaphore(name))

        xres = sb("xres", [128, KD, NT], F32)
        h = sb("h", [128, KD, NT], BF16)
        ybuf = sb("ybuf", [128, NY, NT], BF16)
        ringA = sb("ringA", [128, NA, 2048], BF16)
        ringB = sb("ringB", [128, NB, 1024], BF16)
        ringG = sb("ringG", [128, NGR, 1024], BF16)
        tmp = sb("tmp", [128, NBIG * 1024], F32)
        tmp_bf = tmp.bitcast(BF16)
        vbuf = sb("vbuf", [128, 2, 1032], F32)
        vcb = sb("vcb", [128, 4, NT], BF16)
        par = sb("par_sb", [128, NPAR], F32)
        der = sb("der", [128, NDER], F32)
        scb = sb("scb", [128, KD], BF16)
        ones = sb("ones", [128, 128], BF16)
        cvst = sb("cvst", [128, KE, 2], F32)
        vst = sb("vst", [128, KE, 3], F32)
        hst = sb("hst", [128, KE], F32)
        ident1 = sb("ident1", [33, 1], F32)
        ps = es.enter_context(nc.psum_tensor("ps", [128, 8 * 512], F32))

        eng_sems = {e: sem("s_" + e) for e in Sched.ENGS}
        semA = [sem(f"sA{i}") for i in range(NA)]
        semB = [sem(f"sB{i}") for i in range(NB)]
        semG = [sem(f"sG{i}") for i in range(NGR)]
        semX = [sem(f"sX{i}") for i in range(KD)]
        semO = [sem(f"sO{i}") for i in range(4)]
        semP = sem("sP")

        def bank(b, n=1):
            return ps[:, b * 512:(b + n) * 512]

        def big(i):
            return tmp[:, i * 1024:(i + 1) * 1024]

        def half(i, hq):
            return tmp[:, i * 1024 + hq * 512:i * 1024 + (hq + 1) * 512]

        def xsq_ap(q, lo=0, hi=NT):
            base = 4 * 2048 + q * 1024
            return tmp_bf[:, base + lo:base + hi]

        def flow(S, RA, RB, RG):
            B_x = [[Buf(f"x{k}_{t}") for t in range(NTC)] for k in range(KD)]
            B_h = [Buf(f"h{k}") for k in range(KD)]
            B_y = [Buf(f"y{i}") for i in range(NY)]
            B_A = [Buf(f"A{i}") for i in range(NA)]
            B_B = [Buf(f"B{i}") for i in range(NB)]
            B_G = [Buf(f"G{i}") for i in range(NGR)]
            B_T = [Buf(f"T{i}") for i in range(2 * NBIG)]
            B_vbuf = [Buf("vbuf0"), Buf("vbuf1")]
            B_vcb = [Buf(f"vcb{i}") for i in range(4)]
            B_par, B_der, B_scb, B_ones = Buf("par"), Buf("der"), Buf("scb"), Buf("ones")
            B_ss = [Buf("ss0"), Buf("ss1")]
            B_gt = [Buf("gt0"), Buf("gt1")]
            B_st = [Buf(f"st{i}") for i in range(KE)]
            B_hst = [Buf(f"hst{i}") for i in range(KE)]
            B_ps = [Buf(f"ps{i}") for i in range(8)]
            B_row = [B_T[12], B_T[13]]

            def bb(i):
                return [B_T[2 * i], B_T[2 * i + 1]]

            def issueA(k, s, tile):
                S.op("gpsimd", lambda e, s=s, tile=tile: e.dma_start(out=ringA[:, s, :], in_=tA[tile]),
                     writes=[B_A[s]], dma_sem=semA[s])

            def issueB(k, s, tile):
                S.op("gpsimd", lambda e, s=s, tile=tile: e.dma_start(out=ringB[:, s, :], in_=tB[tile]),
                     writes=[B_B[s]], dma_sem=semB[s])

            def issueG(k, s, tile):
                S.op("gpsimd", lambda e, s=s, tile=tile: e.dma_start(out=ringG[:, s, :], in_=tB[tile]),
                     writes=[B_G[s]], dma_sem=semG[s])

            RA.issue_fn = issueA
            RB.issue_fn = issueB
            RG.issue_fn = issueG
            RA.pump()
            RB.pump()

            S.op("sync", lambda e: e.dma_start(out=par[:], in_=par_d), writes=[B_par], dma_sem=semP)
            S.op("vector", lambda e: e.memset(ones[:], 1.0), writes=[B_ones])
            S.op("vector", lambda e: e.memset(ident1[:], 1.0), writes=[B_ones])
            S.op("vector", lambda e: e.memset(cvst[:], 0.0), writes=B_st)
            S.op("vector", lambda e: e.memset(vst[:], 0.0), writes=B_st)
            S.op("vector", lambda e: e.memset(hst[:], 0.0), writes=B_hst)
            S.op("vector", lambda e: e.memset(der[:, D_EPS:D_EPS + 1], EPS), writes=[B_der])
            S.op("vector", lambda e: e.memset(der[:, D_QTR:D_QTR + 1], 0.0625), writes=[B_der])
            S.op("scalar", lambda e: e.activation(out=der[:, D_CN:D_CN + 32], in_=par[:, P_LAM:P_LAM + 32],
                                                  func=AF.Abs),
                 reads=[B_par], writes=[B_der])
            S.op("scalar", lambda e: e.activation(out=der[:, D_CN:D_CN + 32], in_=der[:, D_CN:D_CN + 32],
                                                  func=AF.Exp, scale=-1.0),
                 reads=[B_der], writes=[B_der])
            S.op("scalar", lambda e: e.activation(out=der[:, D_CN:D_CN + 32], in_=der[:, D_CN:D_CN + 32],
                                                  func=AF.Ln, bias=1.0),
                 reads=[B_der], writes=[B_der])
            S.op("vector", lambda e: e.tensor_scalar(out=der[:, D_HC:D_HC + 32], in0=par[:, P_LAM:P_LAM + 32],
                                                     scalar1=-1.0, scalar2=0.0, op0=ALU.mult, op1=ALU.max),
                 reads=[B_par, B_der], writes=[B_der])
            S.op("vector", lambda e: e.tensor_tensor(out=der[:, D_CN:D_CN + 32], in0=der[:, D_CN:D_CN + 32],
                                                     in1=der[:, D_HC:D_HC + 32], op=ALU.add),
                 reads=[B_der], writes=[B_der])
            S.op("vector", lambda e: e.tensor_scalar(out=der[:, D_HC:D_HC + 32], in0=der[:, D_CN:D_CN + 32],
                                                     scalar1=-4.0, scalar2=None, op0=ALU.mult),
                 reads=[B_der], writes=[B_der])
            S.op("vector", lambda e: e.tensor_scalar(out=der[:, D_CN:D_CN + 32], in0=der[:, D_CN:D_CN + 32],
                                                     scalar1=-8.0, scalar2=None, op0=ALU.mult),
                 reads=[B_der], writes=[B_der])
            S.op("vector", lambda e: e.tensor_scalar(out=der[:, D_HBA:D_HBA + 64], in0=par[:, P_BA:P_BA + 64],
                                                     scalar1=0.5, scalar2=None, op0=ALU.mult),
                 reads=[B_par, B_der], writes=[B_der])
            S.op("scalar", lambda e: e.activation(out=scb[:], in_=par[:, P_C:P_C + 16], func=AF.Silu),
                 reads=[B_par], writes=[B_scb])

            sqc = {"n": 0}

            def stats_chunk(kd, defer=False):
                q = sqc["n"] % 2
                sqc["n"] += 1
                S.op("scalar", lambda e, kd=kd, q=q: e.activation(
                    out=xsq_ap(q), in_=xres[:, kd, :], func=AF.Square),
                     reads=B_x[kd], writes=[B_T[8 + q]])
                if defer:
                    return lambda: stats_mm(kd, q)
                stats_mm(kd, q)

            def stats_mm(kd, q):
                for tc in range(NTC):
                    S.op("tensor", lambda e, kd=kd, q=q, tc=tc: e.matmul(
                        bank(tc), ones[:], xsq_ap(q, tc * TC, (tc + 1) * TC),
                        start=(kd == 0), stop=(kd == KD - 1)),
                         reads=[B_ones, B_T[8 + q]], writes=[B_ps[tc]])

            def rstd_ops():
                S.op("scalar", lambda e: e.activation(out=big(9), in_=bank(0, 2), func=AF.Ln, scale=1.0 / D,
                                                      bias=der[:, D_EPS:D_EPS + 1]),
                     reads=[B_ps[0], B_ps[1], B_der], writes=bb(9))
                S.op("scalar", lambda e: e.activation(out=big(9), in_=big(9), func=AF.Exp, scale=-0.5),
                     reads=bb(9), writes=bb(9))

            def normalize(L):
                dasc = D_ASC0 if L == 0 else D_ASC1
                dmod = D_MOD0 if L == 0 else D_MOD1
                for kd in range(KD):
                    q = kd % 2
                    S.op("vector", lambda e, kd=kd, q=q: e.tensor_tensor(
                        out=big(7 + q), in0=xres[:, kd, :], in1=big(9), op=ALU.mult),
                         reads=B_x[kd] + bb(9), writes=bb(7 + q))
                    S.op("scalar", lambda e, kd=kd, q=q: e.activation(
                        out=h[:, kd, :], in_=big(7 + q), func=AF.Identity,
                        scale=der[:, dasc + kd:dasc + kd + 1], bias=der[:, dmod + kd:dmod + kd + 1]),
                         reads=bb(7 + q) + [B_ss[L]], writes=[B_h[kd]])

            rowc = {"n": 0}

            def ada_mm(ring_t, s, j, kd):
                gp = 32 * (kd % 2)
                S.op("tensor", lambda e: e.matmul(
                    ps[gp:gp + 1, 7 * 512:8 * 512], scb[:, kd:kd + 1], ring_t[:, s, j * 512:(j + 1) * 512],
                    start=(kd < 2), stop=(kd >= KD - 2), tile_position=(0, gp)),
                     reads=[(B_A if ring_t is ringA else B_B)[s], B_scb], writes=[B_ps[7]])

            def ada_finish(L, fc0):
                dmod = D_MOD0 if L == 0 else D_MOD1
                pab = P_AB0 if L == 0 else P_AB1
                r = rowc["n"] % 2
                rowc["n"] += 1
                wr = B_gt[L] if fc0 >= 32 else B_ss[L]
                rb = 6 * 1024 + r * 512
                S.op("vector", lambda e: e.tensor_copy(out=tmp[0:33, rb:rb + 512], in_=ps[0:33, 7 * 512:8 * 512]),
                     reads=[B_ps[7]], writes=[B_row[r]])
                for gq in range(2):
                    for j in range(4):
                        S.op("tensor", lambda e, j=j, gq=gq: e.transpose(
                            ps[:, 7 * 512 + gq * 4 + j:7 * 512 + gq * 4 + j + 1],
                            tmp[32 * gq:32 * gq + 1, rb + j * 128:rb + (j + 1) * 128], ident1[32 * gq:32 * gq + 1, 0:1]),
                             reads=[B_row[r], B_ones], writes=[B_ps[7]])
                S.op("vector", lambda e: e.tensor_tensor(
                    out=der[:, dmod + fc0:dmod + fc0 + 4], in0=ps[:, 7 * 512:7 * 512 + 4],
                    in1=par[:, pab + fc0:pab + fc0 + 4], op=ALU.add),
                     reads=[B_ps[7], B_par], writes=[wr])
                S.op("vector", lambda e: e.tensor_tensor(
                    out=der[:, dmod + fc0:dmod + fc0 + 4], in0=ps[:, 7 * 512 + 4:7 * 512 + 8],
                    in1=der[:, dmod + fc0:dmod + fc0 + 4], op=ALU.add),
                     reads=[B_ps[7], wr], writes=[wr])

            def asc_op(L):
                dmod = D_MOD0 if L == 0 else D_MOD1
                pg = P_G0 if L == 0 else P_G1
                dasc = D_ASC0 if L == 0 else D_ASC1
                S.op("vector", lambda e: e.scalar_tensor_tensor(
                    out=der[:, dasc:dasc + 16], in0=der[:, dmod + 16:dmod + 32], scalar=1.0, in1=par[:, pg:pg + 16],
                    op0=ALU.add, op1=ALU.mult),
                     reads=[B_ss[L], B_par], writes=[B_ss[L]])

            def late_info(fb):
                return (0, 32 + fb * 4) if fb < 4 else (1, (fb - 4) * 4)

            late_groups = [fb for fb in range(16) if late_info(fb)[0] in layers]
            late_pos = {"n": 0}

            def ada_piece(n=1):
                for _ in range(n):
                    i = late_pos["n"]
                    if i >= len(late_groups) * 8:
                        return
                    late_pos["n"] += 1
                    fb = late_groups[i // 8]
                    kp = i % 8
                    k = RB.acquire(B_ADAL + fb * 8 + kp)
                    s = k % NB
                    for j in range(2):
                        ada_mm(ringB, s, j, 2 * kp + j)
                    RB.release(k)
                    if kp == 7:
                        L, fc0 = late_info(fb)
                        ada_finish(L, fc0)
                        if i // 8 == len(late_groups) - 1 and 1 in layers and layers[0] != 1:
                            asc_op(1)

            ocnt = {"n": 0}

            def out_proj(L, g, with_stats):
                bbase = B_OUT0 if L == 0 else B_OUT1
                dmod = D_MOD0 if L == 0 else D_MOD1
                pend = None
                for dc in range(KD):
                    k = RB.acquire(bbase + g * KD + dc)
                    s = k % NB
                    for tc in range(NTC):
                        bk = 6 + (ocnt["n"] % 2)
                        ocnt["n"] += 1
                        for ke in range(G):
                            ys = (g * G + ke) % NY
                            S.op("tensor", lambda e, s=s, ke=ke, ys=ys, tc=tc, bk=bk: e.matmul(
                                bank(bk), ringB[:, s, ke * 128:(ke + 1) * 128], ybuf[:, ys, tc * TC:(tc + 1) * TC],
                                start=(ke == 0), stop=(ke == G - 1)),
                                 reads=[B_B[s], B_y[ys]], writes=[B_ps[bk]])
                        S.op("vector", lambda e, dc=dc, tc=tc, bk=bk: e.scalar_tensor_tensor(
                            out=xres[:, dc, tc * TC:(tc + 1) * TC], in0=bank(bk),
                            scalar=der[:, dmod + 32 + dc:dmod + 33 + dc], in1=xres[:, dc, tc * TC:(tc + 1) * TC],
                            op0=ALU.mult, op1=ALU.add),
                             reads=[B_ps[bk], B_gt[L], B_x[dc][tc]], writes=[B_x[dc][tc]])
                    RB.release(k)
                    if pend is not None:
                        pend()
                        pend = None
                    if with_stats:
                        pend = stats_chunk(dc, defer=True)
                if pend is not None:
                    pend()

            pcnt = {"n": 0}

            def mixer0(tb, stats_after):
                for g in range(NG):
                    for el in range(G):
                        ec = g * G + el
                        q = ec % 2
                        ys = ec % NY
                        kt = {}
                        for j in (1, 2, 0, 3):
                            kt[j] = RA.acquire(A_W0 + ec * 4 + j)
                        sl = {j: kt[j] % NA for j in kt}
                        S.op("scalar", lambda e, ec=ec, q=q: e.activation(
                            out=vbuf[:, q, 0:2], in_=cvst[:, ec, :], func=AF.Copy),
                             reads=[B_st[ec]], writes=[B_vbuf[q]])
                        for tc in range(NTC):
                            o = tc * TC
                            tq = tc
                            if tb == 0:
                                ada_piece()
                            pr = (pcnt["n"] % 3) * 2
                            pcnt["n"] += 1
                            for kd in range(KD):
                                for j, bk in ((1, pr), (2, pr + 1)):
                                    S.op("tensor", lambda e, s=sl[j], kd=kd, o=o, bk=bk: e.matmul(
                                        bank(bk), ringA[:, s, kd * 128:(kd + 1) * 128], h[:, kd, o:o + TC],
                                        start=(kd == 0), stop=(kd == KD - 1)),
                                         reads=[B_A[sl[j]], B_h[kd]], writes=[B_ps[bk]])
                            if tc == NTC - 1:
                                RA.release(kt[1])
                                RA.release(kt[2])
                            S.op("scalar", lambda e, pr=pr, tq=tq: e.activation(
                                out=half(0, tq), in_=bank(pr + 1), func=AF.Copy),
                                 reads=[B_ps[pr + 1]], writes=[B_T[0 + tq]])
                            S.op("vector", lambda e, pr=pr, tq=tq, q=q, o=o: e.tensor_tensor(
                                out=vbuf[:, q, 2 + o:2 + o + TC], in0=bank(pr), in1=half(0, tq), op=ALU.mult),
                                 reads=[B_ps[pr], B_T[0 + tq]], writes=[B_vbuf[q]])
                            S.op("vector", lambda e, q=q, o=o, tq=tq, ec=ec: e.tensor_scalar(
                                out=half(1, tq), in0=vbuf[:, q, 2 + o:2 + o + TC],
                                scalar1=par[:, P_SCW + 64 + ec:P_SCW + 65 + ec], scalar2=None, op0=ALU.mult),
                                 reads=[B_vbuf[q], B_par], writes=[B_T[2 + tq]])
                            for kk, sh in ((1, 1), (0, 0)):
                                S.op("vector", lambda e, q=q, o=o, tq=tq, ec=ec, kk=kk, sh=sh: e.scalar_tensor_tensor(
                                    out=half(1, tq), in0=vbuf[:, q, sh + o:sh + o + TC],
                                    scalar=par[:, P_SCW + kk * 32 + ec:P_SCW + kk * 32 + ec + 1], in1=half(1, tq),
                                    op0=ALU.mult, op1=ALU.add),
                                     reads=[B_vbuf[q], B_par, B_T[2 + tq]], writes=[B_T[2 + tq]])
                            if tb == 0:
                                ada_piece()
                            pr2 = (pcnt["n"] % 3) * 2
                            pcnt["n"] += 1
                            for kd in range(KD):
                                for j, bk in ((0, pr2), (3, pr2 + 1)):
                                    S.op("tensor", lambda e, s=sl[j], kd=kd, o=o, bk=bk: e.matmul(
                                        bank(bk), ringA[:, s, kd * 128:(kd + 1) * 128], h[:, kd, o:o + TC],
                                        start=(kd == 0), stop=(kd == KD - 1)),
                                         reads=[B_A[sl[j]], B_h[kd]], writes=[B_ps[bk]])
                            if tc == NTC - 1:
                                RA.release(kt[0])
                                RA.release(kt[3])
                            S.op("scalar", lambda e, pr2=pr2, tq=tq: e.activation(
                                out=half(2, tq), in_=bank(pr2 + 1), func=AF.Silu),
                                 reads=[B_ps[pr2 + 1]], writes=[B_T[4 + tq]])
                            S.op("vector", lambda e, pr2=pr2, tq=tq: e.tensor_tensor(
                                out=half(3, tq), in0=bank(pr2), in1=half(2, tq), op=ALU.mult),
                                 reads=[B_ps[pr2], B_T[4 + tq]], writes=[B_T[6 + tq]])
                            S.op("vector", lambda e, tq=tq, ys=ys, o=o: e.tensor_tensor(
                                out=ybuf[:, ys, o:o + TC], in0=half(3, tq), in1=half(1, tq), op=ALU.mult),
                                 reads=[B_T[6 + tq], B_T[2 + tq]], writes=[B_y[ys]])
                        S.op("scalar", lambda e, ec=ec, q=q: e.activation(
                            out=cvst[:, ec, :], in_=vbuf[:, q, NT:NT + 2], func=AF.Copy),
                             reads=[B_vbuf[q]], writes=[B_st[ec]])
                        if el == 1 and g > 0:
                            out_proj(0, g - 1, False)
                if tb == 0:
                    ada_piece(16 * 8)
                out_proj(0, NG - 1, stats_after)

            def mixer1(stats_after):
                def Vstep(ec):
                    q = ec % 2
                    vs = ec % 3
                    cs = ec % 4
                    k = RA.acquire(A_W1 + ec * 2 + 0)
                    s = k % NA
                    S.op("scalar", lambda e: e.activation(out=vbuf[:, q, 0:3], in_=vst[:, ec, :], func=AF.Copy),
                         reads=[B_st[ec]], writes=[B_vbuf[q]])
                    for kd in range(KD):
                        for tc in range(NTC):
                            S.op("tensor", lambda e, kd=kd, tc=tc: e.matmul(
                                bank(tc), ringA[:, s, kd * 128:(kd + 1) * 128], h[:, kd, tc * TC:(tc + 1) * TC],
                                start=(kd == 0), stop=(kd == KD - 1)),
                                 reads=[B_A[s], B_h[kd]], writes=[B_ps[tc]])
                    RA.release(k)
                    S.op("scalar", lambda e: e.activation(out=vbuf[:, q, 3:3 + NT], in_=bank(0, 2), func=AF.Copy),
                         reads=[B_ps[0], B_ps[1]], writes=[B_vbuf[q]])
                    S.op("scalar", lambda e: e.activation(
                        out=big(vs), in_=vbuf[:, q, 3:3 + NT], func=AF.Identity,
                        scale=par[:, P_LCW + 96 + ec:P_LCW + 97 + ec], bias=par[:, P_LCB + ec:P_LCB + ec + 1]),
                         reads=[B_vbuf[q], B_par], writes=bb(vs))
                    for kk, eng in ((2, "vector"), (1, "vector"), (0, "vector")):
                        S.op(eng, lambda e, kk=kk: e.scalar_tensor_tensor(
                            out=big(vs), in0=vbuf[:, q, kk:kk + NT],
                            scalar=par[:, P_LCW + kk * 32 + ec:P_LCW + kk * 32 + ec + 1], in1=big(vs),
                            op0=ALU.mult, op1=ALU.add),
                             reads=[B_vbuf[q], B_par] + bb(vs), writes=bb(vs))
                    S.op("vector", lambda e: e.tensor_copy(out=vcb[:, cs, :], in_=big(vs)),
                         reads=bb(vs), writes=[B_vcb[cs]])
                    S.op("scalar", lambda e: e.activation(out=vst[:, ec, :], in_=vbuf[:, q, NT:NT + 3], func=AF.Copy),
                         reads=[B_vbuf[q]], writes=[B_st[ec]])

                def Gstep(ec, low_pair=False):
                    gq = ec % 2
                    pb = 2 if (gq == 0 or low_pair) else 6
                    k = RA.acquire(A_W1 + ec * 2 + 1)
                    s = k % NA
                    for kd in range(KD):
                        for tc in range(NTC):
                            S.op("tensor", lambda e, kd=kd, tc=tc: e.matmul(
                                bank(pb + tc), ringA[:, s, kd * 128:(kd + 1) * 128], h[:, kd, tc * TC:(tc + 1) * TC],
                                start=(kd == 0), stop=(kd == KD - 1)),
                                 reads=[B_A[s], B_h[kd]], writes=[B_ps[pb + tc]])
                    RA.release(k)
                    S.op("scalar", lambda e: e.activation(out=big(3 + gq), in_=bank(pb, 2), func=AF.Tanh, scale=0.5),
                         reads=[B_ps[pb], B_ps[pb + 1]], writes=bb(3 + gq))
                    S.op("vector", lambda e: e.scalar_tensor_tensor(
                        out=big(3 + gq), in0=big(3 + gq), scalar=1.0, in1=bank(pb, 2), op0=ALU.add, op1=ALU.mult),
                         reads=bb(3 + gq) + [B_ps[pb], B_ps[pb + 1]], writes=bb(3 + gq))

                def gate_mm(hd, oc, ax, sG):
                    for tc in range(NTC):
                        for ic in range(2):
                            col = ((ax * 2 + ic) * 2 + oc) * 128
                            cs = (2 * hd + ic) % 4
                            S.op("tensor", lambda e, col=col, ic=ic, tc=tc, cs=cs: e.matmul(
                                bank(4 + tc), ringG[:, sG, col:col + 128], vcb[:, cs, tc * TC:(tc + 1) * TC],
                                start=(ic == 0), stop=(ic == 1)),
                                 reads=[B_G[sG], B_vcb[cs]], writes=[B_ps[4 + tc]])

                def Rstep(hd, oc, sG):
                    ec = 2 * hd + oc
                    rq = ec % 2
                    gate_mm(hd, oc, 0, sG)
                    S.op("scalar", lambda e: e.activation(out=big(5 + rq), in_=bank(4, 2), func=AF.Tanh, scale=0.5,
                                                          bias=der[:, D_HBA + ec:D_HBA + ec + 1]),
                         reads=[B_ps[4], B_ps[5], B_der], writes=bb(5 + rq))
                    S.op("scalar", lambda e: e.activation(out=big(8), in_=big(5 + rq), func=AF.Exp,
                                                          scale=der[:, D_CN + ec:D_CN + ec + 1],
                                                          bias=der[:, D_CN + ec:D_CN + ec + 1]),
                         reads=bb(5 + rq) + [B_der], writes=bb(8))
                    S.op("scalar", lambda e: e.activation(out=big(5 + rq), in_=big(5 + rq), func=AF.Exp,
                                                          scale=der[:, D_HC + ec:D_HC + ec + 1],
                                                          bias=der[:, D_HC + ec:D_HC + ec + 1]),
                         reads=bb(5 + rq) + [B_der], writes=bb(5 + rq))
                    S.op("scalar", lambda e: e.activation(out=big(8), in_=big(8), func=AF.Sqrt, scale=-0.0625,
                                                          bias=der[:, D_QTR:D_QTR + 1]),
                         reads=bb(8) + [B_der], writes=bb(8))

                def Istep(hd, oc, sG):
                    ec = 2 * hd + oc
                    rq = ec % 2
                    gq = ec % 2
                    vs = ec % 3
                    ys = ec % NY
                    gate_mm(hd, oc, 1, sG)
                    S.op("scalar", lambda e: e.activation(out=big(7), in_=bank(4, 2), func=AF.Tanh, scale=0.5,
                                                          bias=der[:, D_HBX + ec:D_HBX + ec + 1]),
                         reads=[B_ps[4], B_ps[5], B_der], writes=bb(7))
                    S.op("vector", lambda e: e.scalar_tensor_tensor(
                        out=big(7), in0=big(7), scalar=1.0, in1=big(vs), op0=ALU.add, op1=ALU.mult),
                         reads=bb(7) + bb(vs), writes=bb(7))
                    S.op("vector", lambda e: e.tensor_tensor(out=big(7), in0=big(8), in1=big(7), op=ALU.mult),
                         reads=bb(8) + bb(7), writes=bb(7))
                    S.op("vector", lambda e: e.tensor_tensor_scan(
                        out=big(9), data0=big(5 + rq), data1=big(7), initial=hst[:, ec:ec + 1],
                        op0=ALU.mult, op1=ALU.add),
                         reads=bb(5 + rq) + bb(7) + [B_hst[ec]], writes=bb(9))
                    S.op("gpsimd", lambda e: e.tensor_tensor(
                        out=ybuf[:, ys, :], in0=big(9), in1=big(3 + gq), op=ALU.mult),
                         reads=bb(9) + bb(3 + gq), writes=[B_y[ys]])
                    S.op("vector", lambda e: e.tensor_copy(
                        out=hst[:, ec:ec + 1], in_=tmp[:, 9 * 1024 + NT - 1:9 * 1024 + NT]),
                         reads=bb(9), writes=[B_hst[ec]])

                NH = KE // 2
                Vstep(0)
                Vstep(1)
                for hd in range(NH):
                    g = hd // (G // 2)
                    kG = RG.acquire(B_GATE + hd)
                    sG = kG % NGR
                    Gstep(2 * hd)
                    Rstep(hd, 0, sG)
                    if hd + 1 < NH:
                        Vstep(2 * hd + 2)
                    Istep(hd, 0, sG)
                    Gstep(2 * hd + 1, low_pair=(hd % (G // 2) == 0 and hd > 0))
                    Rstep(hd, 1, sG)
                    if hd % (G // 2) == 0 and hd > 0:
                        out_proj(1, g - 1, False)
                    if hd + 1 < NH:
                        Vstep(2 * hd + 3)
                    Istep(hd, 1, sG)
                    RG.release(kG)
                out_proj(1, NG - 1, stats_after)

            store_ops = []

            def load_x(tb, kd, stats=True):
                S.op("sync", lambda e, kd=kd, tb=tb: e.dma_start(
                    out=xres[:, kd, :], in_=xT[kd * 128:(kd + 1) * 128, tb * NT:(tb + 1) * NT]),
                     writes=B_x[kd], dma_sem=semX[kd])
                if stats:
                    stats_chunk(kd)

            def final_phase(tb, load_next):
                if final_norm:
                    rstd_ops()
                LAG = 3
                for kd in range(KD):
                    q = kd % 4
                    if final_norm:
                        S.op("vector", lambda e, kd=kd, q=q: e.scalar_tensor_tensor(
                            out=big(5 + q), in0=xres[:, kd, :], scalar=par[:, P_FG + kd:P_FG + kd + 1], in1=big(9),
                            op0=ALU.mult, op1=ALU.mult),
                             reads=B_x[kd] + bb(9) + [B_par], writes=bb(5 + q))
                    else:
                        S.op("scalar", lambda e, kd=kd, q=q: e.activation(
                            out=big(5 + q), in_=xres[:, kd, :], func=AF.Copy),
                             reads=B_x[kd], writes=bb(5 + q))
                    st = S.op("sync", lambda e, kd=kd, q=q, tb=tb: e.dma_start(
                        out=outT[kd * 128:(kd + 1) * 128, tb * NT:(tb + 1) * NT], in_=big(5 + q)),
                              reads=bb(5 + q), dma_sem=semO[q])
                    store_ops.append(st)
                    if load_next:
                        load_x(tb + 1, kd, stats=False)
                        if kd >= LAG:
                            stats_chunk(kd - LAG)
                if load_next:
                    for kd in range(KD - LAG, KD):
                        stats_chunk(kd)

            for kd in range(KD):
                load_x(0, kd)
            L0 = layers[0]
            for fb in range(8):
                for kq in range(4):
                    k = RA.acquire(A_ADA0 + fb * 4 + kq)
                    s_ = k % NA
                    for j in range(4):
                        ada_mm(ringA, s_, j, 4 * kq + j)
                    RA.release(k)
                ada_finish(L0, fb * 4)
            asc_op(L0)
            for tb in range(nblk):
                for li, L in enumerate(layers):
                    rstd_ops()
                    normalize(L)
                    last = (li == len(layers) - 1)
                    stats_after = (not last) or final_norm
                    if L == 0:
                        mixer0(tb, stats_after)
                    else:
                        if tb == 0:
                            ada_piece(16 * 8)
                        mixer1(stats_after)
                final_phase(tb, tb + 1 < nblk)
            S.op("sync", lambda e: e.nop(), extra=store_ops)

        RA0, RB0, RG0 = Ring(NA, None), Ring(NB, None), Ring(NGR, None)
        flow(Sched(dry=True), RA0, RB0, RG0)
        S = Sched()
        flow(S, Ring(NA, RA0.rec), Ring(NB, RB0.rec), Ring(NGR, RG0.rec))
        S.finalize(eng_sems)
        with nc.Block() as block:
            @block.tensor
            def _(e):
                S.emit("tensor", e)

            @block.scalar
            def _(e):
                S.emit("scalar", e)

            @block.vector
            def _(e):
                S.emit("vector", e)

            @block.gpsimd
            def _(e):
                S.emit("gpsimd", e)

            @block.sync
            def _(e):
                S.emit("sync", e)
    return nc


def _tilesA(W):
    F = W.shape[1]
    return np.ascontiguousarray(W.reshape(16, 128, F // 128, 128).transpose(2, 1, 0, 3)).reshape(F // 128, 128, 2048)


def _cols(v):
    v = np.asarray(v, np.float32).reshape(-1, 128)
    return v.T


def prep_shared(inp, first_layer=0):
    tA = np.empty((NTILE_A, 128, 2048), np.float32)
    We = inp["ada_w"][first_layer][:, :4096]
    tA[A_ADA0:A_ADA0 + 32] = np.ascontiguousarray(
        We.reshape(4, 4, 128, 8, 512).transpose(3, 0, 2, 1, 4)).reshape(32, 128, 2048)
    w0 = _tilesA(inp["sc_w_in"][0])
    tA[A_W0:A_W0 + 128] = w0.reshape(4, 32, 128, 2048).transpose(1, 0, 2, 3).reshape(128, 128, 2048)
    w1 = _tilesA(inp["lru_w_in"][0])
    tA[A_W1:A_W1 + 64] = w1.reshape(2, 32, 128, 2048).transpose(1, 0, 2, 3).reshape(64, 128, 2048)

    tB = np.empty((NTILE_B, 128, 1024), np.float32)

    def wout(W):
        return np.ascontiguousarray(W.reshape(NG, G, 128, KD, 128).transpose(0, 3, 2, 1, 4)).reshape(NG * KD, 128, 1024)

    tB[B_OUT0:B_OUT0 + 64] = wout(inp["sc_w_out"][0])
    tB[B_OUT1:B_OUT1 + 64] = wout(inp["lru_w_out"][0])
    wa = inp["lru_w_a"][0].reshape(16, 2, 128, 2, 128)
    wx = inp["lru_w_x"][0].reshape(16, 2, 128, 2, 128)
    gt = np.stack([wa, wx], axis=0)
    tB[B_GATE:B_GATE + 16] = np.ascontiguousarray(gt.transpose(1, 3, 0, 2, 4, 5)).reshape(16, 128, 1024)
    Wl = np.concatenate([inp["ada_w"][0][:, 4096:], inp["ada_w"][1]], axis=1)
    tB[B_ADAL:B_ADAL + 128] = np.ascontiguousarray(
        Wl.reshape(8, 2, 128, 16, 512).transpose(3, 0, 2, 1, 4)).reshape(128, 128, 1024)

    par = np.zeros((128, NPAR), np.float32)
    par[:, P_G0:P_G0 + 16] = _cols(inp["norm_g"][0])
    par[:, P_G1:P_G1 + 16] = _cols(inp["norm_g"][1])
    par[:, P_FG:P_FG + 16] = _cols(inp["final_g"])
    par[:, P_AB0:P_AB0 + 48] = _cols(inp["ada_b"][0])
    par[:, P_AB1:P_AB1 + 48] = _cols(inp["ada_b"][1])
    for k in range(3):
        par[:, P_SCW + k * 32:P_SCW + (k + 1) * 32] = _cols(inp["sc_conv_w"][0][k])
    for k in range(4):
        par[:, P_LCW + k * 32:P_LCW + (k + 1) * 32] = _cols(inp["lru_conv_w"][0][k])
    par[:, P_LCB:P_LCB + 32] = _cols(inp["lru_conv_b"][0])
    par[:, P_BA:P_BA + 32] = _cols(inp["lru_b_a"][0])
    par[:, P_BX:P_BX + 32] = _cols(inp["lru_b_x"][0])
    par[:, P_LAM:P_LAM + 32] = _cols(inp["lru_lambda"][0])
    return tA, tB, par


def kernel(**inputs):
    inp = {k: np.asarray(v) for k, v in inputs.items()}
    tA, tB, par = prep_shared(inp)
    x = inp["x"]
    c = inp["c"]
    nb = x.shape[0]
    in_maps = []
    for b in range(nb):
        p = par.copy()
        p[:, P_C:P_C + 16] = _cols(c[b])
        in_maps.append({"xT": np.ascontiguousarray(x[b].T), "tA": tA, "tB": tB, "par": p})
    nc = build_program()
    res = run_bass_kernel_spmd(nc, in_maps, core_ids=list(range(nb)))
    out = np.empty_like(x)
    for b in range(nb):
        out[b] = res.results[b]["outT"].T
    return out
```

```python
import numpy as np
from contextlib import ExitStack
import concourse.bass as bass
import concourse.mybir as mybir
from concourse.bass_utils import run_bass_kernel_spmd

F32 = mybir.dt.float32
BF16 = mybir.dt.bfloat16
AF = mybir.ActivationFunctionType
ALU = mybir.AluOpType

D = 2048
E = 4096
T = 2048
KD = D // 128
KE = E // 128
NT = 1024
NBLK = T // NT
TC = 512
NTC = NT // TC
G = 8
NG = KE // G
EPS = 1e-6

NA = 5
NB = 3
NGR = 2
NY = 10
NBIG = 10

A_ADA0, A_W0, A_W1 = 0, 32, 160
NTILE_A = 224
B_OUT0, B_GATE, B_OUT1, B_ADAL = 0, 64, 80, 144
NTILE_B = 272

P_G0, P_G1, P_FG, P_AB0, P_AB1, P_SCW, P_LCW, P_LCB, P_BA, P_BX, P_LAM, P_C = (
    0, 16, 32, 48, 96, 144, 240, 368, 400, 432, 464, 496)
NPAR = 512
D_ASC0, D_ASC1, D_CN, D_HC, D_HBA, D_HBX, D_MOD0, D_MOD1 = 0, 16, 32, 64, 96, 128, 160, 208
NDER = 264
D_EPS, D_QTR = 256, 257


class Buf:
    __slots__ = ("name", "w", "rs", "rdma")

    def __init__(self, name):
        self.name = name
        self.w = None
        self.rs = {}
        self.rdma = []


class Op:
    __slots__ = ("eng", "fn", "deps", "sem", "inc", "idx", "val", "need", "is_dma")


class Sched:
    ENGS = ("tensor", "scalar", "vector", "gpsimd", "sync")

    def __init__(self, dry=False):
        self.q = {e: [] for e in self.ENGS}
        self.dma_count = {}
        self.dry = dry

    def op(self, eng, fn, reads=(), writes=(), dma_sem=None, extra=()):
        if self.dry:
            return None
        o = Op()
        o.eng = eng
        o.fn = fn
        o.is_dma = dma_sem is not None
        o.sem = dma_sem
        o.inc = 16 if o.is_dma else 1
        o.need = o.is_dma
        o.val = None
        deps = []
        for b in reads:
            if b.w is not None:
                deps.append(b.w)
        for b in writes:
            if b.w is not None:
                deps.append(b.w)
            deps.extend(b.rs.values())
            deps.extend(b.rdma)
        deps.extend(extra)
        o.deps = deps
        o.idx = len(self.q[eng])
        self.q[eng].append(o)
        for b in reads:
            if o.is_dma:
                b.rdma.append(o)
            else:
                b.rs[eng] = o
        for b in writes:
            b.w = o
            b.rs = {}
            b.rdma = []
        return o

    def finalize(self, eng_sems):
        for e in self.ENGS:
            for o in self.q[e]:
                for d in o.deps:
                    if d.is_dma:
                        continue
                    if d.eng == o.eng and d.eng == "tensor" and not o.is_dma:
                        continue
                    if d.eng != o.eng or o.is_dma or (o.idx - d.idx <= 2):
                        d.need = True
        for e in self.ENGS:
            cnt = 0
            for o in self.q[e]:
                if o.is_dma:
                    c = self.dma_count.get(id(o.sem), 0) + 16
                    self.dma_count[id(o.sem)] = c
                    o.val = c
                elif o.need:
                    cnt += 1
                    o.val = cnt
                    o.sem = eng_sems[e]

    def emit(self, eng_name, eng):
        waited = {}
        for o in self.q[eng_name]:
            ws = {}
            for d in o.deps:
                if not d.need:
                    continue
                if (not d.is_dma) and d.eng == o.eng and not o.is_dma and (
                        (o.idx - d.idx > 2) or d.eng == "tensor"):
                    continue
                k = id(d.sem)
                if k not in ws or ws[k][1] < d.val:
                    ws[k] = (d.sem, d.val)
            for k, (sem, val) in ws.items():
                if waited.get(k, 0) >= val:
                    continue
                eng.wait_ge(sem, val)
                waited[k] = val
            ins = o.fn(eng)
            if o.need:
                ins.then_inc(o.sem, o.inc)


class Ring:
    def __init__(self, n, seq):
        self.n = n
        self.seq = seq
        self.rec = []
        self.issued = 0
        self.consumed = 0
        self.rel = set()
        self.issue_fn = None

    def pump(self):
        if self.seq is None:
            return
        while self.issued < len(self.seq) and (self.issued < self.n or (self.issued - self.n) in self.rel):
            k = self.issued
            self.issue_fn(k, k % self.n, self.seq[k])
            self.issued += 1

    def acquire(self, tile):
        if self.seq is None:
            self.rec.append(tile)
            return len(self.rec) - 1
        k = self.consumed
        self.consumed += 1
        assert self.seq[k] == tile, (k, self.seq[k], tile)
        self.pump()
        assert self.issued > k, "ring deadlock: tile needed before its slot was released"
        return k

    def release(self, k):
        if self.seq is None:
            return
        self.rel.add(k)
        self.pump()


def build_program(layers=(0, 1), final_norm=True, nblk=NBLK):
    nc = bass.Bass("TRN2", target_bir_lowering=False)
    xT = nc.dram_tensor("xT", [D, T], F32, kind="ExternalInput").ap()
    tA = nc.dram_tensor("tA", [NTILE_A, 128, 2048], F32, kind="ExternalInput").ap()
    tB = nc.dram_tensor("tB", [NTILE_B, 128, 1024], F32, kind="ExternalInput").ap()
    par_d = nc.dram_tensor("par", [128, NPAR], F32, kind="ExternalInput").ap()
    outT = nc.dram_tensor("outT", [D, T], F32, kind="ExternalOutput").ap()

    with ExitStack() as es:
        def sb(name, shape, dt):
            return es.enter_context(nc.sbuf_tensor(name, shape, dt))

        def sem(name):
            return es.enter_context(nc.semaphore(name))

        xres = sb("xres", [128, KD, NT], F32)
        h = sb("h", [128, KD, NT], BF16)
        ybuf = sb("ybuf", [128, NY, NT], BF16)
        ringA = sb("ringA", [128, NA, 2048], BF16)
        ringB = sb("ringB", [128, NB, 1024], BF16)
        ringG = sb("ringG", [128, NGR, 1024], BF16)
        tmp = sb("tmp", [128, NBIG * 1024], F32)
        tmp_bf = tmp.bitcast(BF16)
        vbuf = sb("vbuf", [128, 2, 1032], F32)
        vcb = sb("vcb", [128, 4, NT], BF16)
        par = sb("par_sb", [128, NPAR], F32)
        der = sb("der", [128, NDER], F32)
        scb = sb("scb", [128, KD], BF16)
        ones = sb("ones", [128, 128], BF16)
        cvst = sb("cvst", [128, KE, 2], F32)
        vst = sb("vst", [128, KE, 3], F32)
        hst = sb("hst", [128, KE], F32)
        ident1 = sb("ident1", [1, 1], F32)
        ps = es.enter_context(nc.psum_tensor("ps", [128, 8 * 512], F32))

        eng_sems = {e: sem("s_" + e) for e in Sched.ENGS}
        semA = [sem(f"sA{i}") for i in range(NA)]
        semB = [sem(f"sB{i}") for i in range(NB)]
        semG = [sem(f"sG{i}") for i in range(NGR)]
        semX = [sem(f"sX{i}") for i in range(KD)]
        semO = [sem(f"sO{i}") for i in range(4)]
        semP = sem("sP")

        def bank(b, n=1):
            return ps[:, b * 512:(b + n) * 512]

        def big(i):
            return tmp[:, i * 1024:(i + 1) * 1024]

        def half(i, hq):
            return tmp[:, i * 1024 + hq * 512:i * 1024 + (hq + 1) * 512]

        def xsq_ap(q, lo=0, hi=NT):
            base = 4 * 2048 + q * 1024
            return tmp_bf[:, base + lo:base + hi]

        def flow(S, RA, RB, RG):
            B_x = [[Buf(f"x{k}_{t}") for t in range(NTC)] for k in range(KD)]
            B_h = [[Buf(f"h{k}_{t}") for t in range(NTC)] for k in range(KD)]
            B_y = [Buf(f"y{i}") for i in range(NY)]
            B_A = [Buf(f"A{i}") for i in range(NA)]
            B_B = [Buf(f"B{i}") for i in range(NB)]
            B_G = [Buf(f"G{i}") for i in range(NGR)]
            B_T = [Buf(f"T{i}") for i in range(2 * NBIG)]
            B_vbuf = [Buf("vbuf0"), Buf("vbuf1")]
            B_vcb = [Buf(f"vcb{i}") for i in range(4)]
            B_par, B_der, B_scb, B_ones = Buf("par"), Buf("der"), Buf("scb"), Buf("ones")
            B_ss = [Buf("ss0"), Buf("ss1")]
            B_gt = [Buf("gt0"), Buf("gt1")]
            B_st = [Buf(f"st{i}") for i in range(KE)]
            B_hst = [Buf(f"hst{i}") for i in range(KE)]
            B_ps = [Buf(f"ps{i}") for i in range(8)]
            B_row = [B_T[12], B_T[13]]

            def bb(i):
                return [B_T[2 * i], B_T[2 * i + 1]]

            def issueA(k, s, tile):
                S.op("gpsimd", lambda e, s=s, tile=tile: e.dma_start(out=ringA[:, s, :], in_=tA[tile]),
                     writes=[B_A[s]], dma_sem=semA[s])

            def issueB(k, s, tile):
                S.op("gpsimd", lambda e, s=s, tile=tile: e.dma_start(out=ringB[:, s, :], in_=tB[tile]),
                     writes=[B_B[s]], dma_sem=semB[s])

            def issueG(k, s, tile):
                S.op("gpsimd", lambda e, s=s, tile=tile: e.dma_start(out=ringG[:, s, :], in_=tB[tile]),
                     writes=[B_G[s]], dma_sem=semG[s])

            RA.issue_fn = issueA
            RB.issue_fn = issueB
            RG.issue_fn = issueG
            RA.pump()
            RB.pump()

            S.op("sync", lambda e: e.dma_start(out=par[:], in_=par_d), writes=[B_par], dma_sem=semP)
            S.op("vector", lambda e: e.memset(ones[:], 1.0), writes=[B_ones])
            S.op("vector", lambda e: e.memset(ident1[:], 1.0), writes=[B_ones])
            S.op("vector", lambda e: e.memset(cvst[:], 0.0), writes=B_st)
            S.op("vector", lambda e: e.memset(vst[:], 0.0), writes=B_st)
            S.op("vector", lambda e: e.memset(hst[:], 0.0), writes=B_hst)
            S.op("vector", lambda e: e.memset(der[:, D_EPS:D_EPS + 1], EPS), writes=[B_der])
            S.op("vector", lambda e: e.memset(der[:, D_QTR:D_QTR + 1], 0.0625), writes=[B_der])
            S.op("scalar", lambda e: e.activation(out=der[:, D_CN:D_CN + 32], in_=par[:, P_LAM:P_LAM + 32],
                                                  func=AF.Abs),
                 reads=[B_par], writes=[B_der])
            S.op("scalar", lambda e: e.activation(out=der[:, D_CN:D_CN + 32], in_=der[:, D_CN:D_CN + 32],
                                                  func=AF.Exp, scale=-1.0),
                 reads=[B_der], writes=[B_der])
            S.op("scalar", lambda e: e.activation(out=der[:, D_CN:D_CN + 32], in_=der[:, D_CN:D_CN + 32],
                                                  func=AF.Ln, bias=1.0),
                 reads=[B_der], writes=[B_der])
            S.op("vector", lambda e: e.tensor_scalar(out=der[:, D_HC:D_HC + 32], in0=par[:, P_LAM:P_LAM + 32],
                                                     scalar1=-1.0, scalar2=0.0, op0=ALU.mult, op1=ALU.max),
                 reads=[B_par, B_der], writes=[B_der])
            S.op("vector", lambda e: e.tensor_tensor(out=der[:, D_CN:D_CN + 32], in0=der[:, D_CN:D_CN + 32],
                                                     in1=der[:, D_HC:D_HC + 32], op=ALU.add),
                 reads=[B_der], writes=[B_der])
            S.op("vector", lambda e: e.tensor_scalar(out=der[:, D_HC:D_HC + 32], in0=der[:, D_CN:D_CN + 32],
                                                     scalar1=-4.0, scalar2=None, op0=ALU.mult),
                 reads=[B_der], writes=[B_der])
            S.op("vector", lambda e: e.tensor_scalar(out=der[:, D_CN:D_CN + 32], in0=der[:, D_CN:D_CN + 32],
                                                     scalar1=-8.0, scalar2=None, op0=ALU.mult),
                 reads=[B_der], writes=[B_der])
            S.op("vector", lambda e: e.tensor_scalar(out=der[:, D_HBA:D_HBA + 64], in0=par[:, P_BA:P_BA + 64],
                                                     scalar1=0.5, scalar2=None, op0=ALU.mult),
                 reads=[B_par, B_der], writes=[B_der])
            S.op("scalar", lambda e: e.activation(out=scb[:], in_=par[:, P_C:P_C + 16], func=AF.Silu),
                 reads=[B_par], writes=[B_scb])

            sqc = {"n": 0}

            def stats_chunk(kd, defer=False):
                q = sqc["n"] % 2
                sqc["n"] += 1
                S.op("scalar", lambda e, kd=kd, q=q: e.activation(
                    out=xsq_ap(q), in_=xres[:, kd, :], func=AF.Square),
                     reads=B_x[kd], writes=[B_T[8 + q]])
                if defer:
                    return lambda: stats_mm(kd, q)
                stats_mm(kd, q)

            def stats_mm(kd, q):
                for tc in range(NTC):
                    S.op("tensor", lambda e, kd=kd, q=q, tc=tc: e.matmul(
                        bank(tc), ones[:], xsq_ap(q, tc * TC, (tc + 1) * TC),
                        start=(kd == 0), stop=(kd == KD - 1)),
                         reads=[B_ones, B_T[8 + q]], writes=[B_ps[tc]])

            def rstd_ops():
                S.op("scalar", lambda e: e.activation(out=big(9), in_=bank(0, 2), func=AF.Ln, scale=1.0 / D,
                                                      bias=der[:, D_EPS:D_EPS + 1]),
                     reads=[B_ps[0], B_ps[1], B_der], writes=bb(9))
                S.op("scalar", lambda e: e.activation(out=big(9), in_=big(9), func=AF.Exp, scale=-0.5),
                     reads=bb(9), writes=bb(9))

            def normalize(L):
                dasc = D_ASC0 if L == 0 else D_ASC1
                dmod = D_MOD0 if L == 0 else D_MOD1
                for tc in range(NTC):
                    o = tc * TC
                    for kd in range(KD):
                        q = kd % 4
                        bi, hq = 7 + q // 2, q % 2
                        S.op("vector", lambda e, kd=kd, bi=bi, hq=hq, o=o, tc=tc: e.tensor_tensor(
                            out=half(bi, hq), in0=xres[:, kd, o:o + TC], in1=half(9, tc), op=ALU.mult),
                             reads=[B_x[kd][tc], B_T[18 + tc]], writes=[B_T[2 * bi + hq]])
                        S.op("scalar", lambda e, kd=kd, bi=bi, hq=hq, o=o: e.activation(
                            out=h[:, kd, o:o + TC], in_=half(bi, hq), func=AF.Identity,
                            scale=der[:, dasc + kd:dasc + kd + 1], bias=der[:, dmod + kd:dmod + kd + 1]),
                             reads=[B_T[2 * bi + hq], B_ss[L]], writes=[B_h[kd][tc]])

            rowc = {"n": 0}

            def ada_mm(ring_t, s, j, kd):
                S.op("tensor", lambda e: e.matmul(
                    ps[0:1, 7 * 512:8 * 512], scb[:, kd:kd + 1], ring_t[:, s, j * 512:(j + 1) * 512],
                    start=(kd == 0), stop=(kd == KD - 1)),
                     reads=[(B_A if ring_t is ringA else B_B)[s], B_scb], writes=[B_ps[7]])

            def ada_finish(L, fc0):
                dmod = D_MOD0 if L == 0 else D_MOD1
                pab = P_AB0 if L == 0 else P_AB1
                r = rowc["n"] % 2
                rowc["n"] += 1
                wr = B_gt[L] if fc0 >= 32 else B_ss[L]
                S.op("vector", lambda e: e.tensor_copy(out=tmp[0:1, 6 * 1024 + r * 512:6 * 1024 + (r + 1) * 512], in_=ps[0:1, 7 * 512:8 * 512]),
                     reads=[B_ps[7]], writes=[B_row[r]])
                for j in range(4):
                    S.op("tensor", lambda e, j=j: e.transpose(
                        ps[:, 7 * 512 + j:7 * 512 + j + 1], tmp[0:1, 6 * 1024 + r * 512 + j * 128:6 * 1024 + r * 512 + (j + 1) * 128], ident1[0:1, 0:1]),
                         reads=[B_row[r], B_ones], writes=[B_ps[7]])
                S.op("vector", lambda e: e.tensor_tensor(
                    out=der[:, dmod + fc0:dmod + fc0 + 4], in0=ps[:, 7 * 512:7 * 512 + 4],
                    in1=par[:, pab + fc0:pab + fc0 + 4], op=ALU.add),
                     reads=[B_ps[7], B_par], writes=[wr])

            def asc_op(L):
                dmod = D_MOD0 if L == 0 else D_MOD1
                pg = P_G0 if L == 0 else P_G1
                dasc = D_ASC0 if L == 0 else D_ASC1
                S.op("vector", lambda e: e.scalar_tensor_tensor(
                    out=der[:, dasc:dasc + 16], in0=der[:, dmod + 16:dmod + 32], scalar=1.0, in1=par[:, pg:pg + 16],
                    op0=ALU.add, op1=ALU.mult),
                     reads=[B_ss[L], B_par], writes=[B_ss[L]])

            def late_info(fb):
                return (0, 32 + fb * 4) if fb < 4 else (1, (fb - 4) * 4)

            late_groups = [fb for fb in range(16) if late_info(fb)[0] in layers]
            late_pos = {"n": 0}

            def ada_piece(n=1):
                for _ in range(n):
                    i = late_pos["n"]
                    if i >= len(late_groups) * 8:
                        return
                    late_pos["n"] += 1
                    fb = late_groups[i // 8]
                    kp = i % 8
                    k = RB.acquire(B_ADAL + fb * 8 + kp)
                    s = k % NB
                    for j in range(2):
                        ada_mm(ringB, s, j, 2 * kp + j)
                    RB.release(k)
                    if kp == 7:
                        L, fc0 = late_info(fb)
                        ada_finish(L, fc0)
                        if i // 8 == len(late_groups) - 1 and 1 in layers and layers[0] != 1:
                            asc_op(1)

            ocnt = {"n": 0}

            def out_proj(L, g, with_stats):
                bbase = B_OUT0 if L == 0 else B_OUT1
                dmod = D_MOD0 if L == 0 else D_MOD1
                pend = None
                for dc in range(KD):
                    k = RB.acquire(bbase + g * KD + dc)
                    s = k % NB
                    for tc in range(NTC):
                        bk = 6 + (ocnt["n"] % 2)
                        ocnt["n"] += 1
                        for ke in range(G):
                            ys = (g * G + ke) % NY
                            S.op("tensor", lambda e, s=s, ke=ke, ys=ys, tc=tc, bk=bk: e.matmul(
                                bank(bk), ringB[:, s, ke * 128:(ke + 1) * 128], ybuf[:, ys, tc * TC:(tc + 1) * TC],
                                start=(ke == 0), stop=(ke == G - 1)),
                                 reads=[B_B[s], B_y[ys]], writes=[B_ps[bk]])
                        S.op("vector", lambda e, dc=dc, tc=tc, bk=bk: e.scalar_tensor_tensor(
                            out=xres[:, dc, tc * TC:(tc + 1) * TC], in0=bank(bk),
                            scalar=der[:, dmod + 32 + dc:dmod + 33 + dc], in1=xres[:, dc, tc * TC:(tc + 1) * TC],
                            op0=ALU.mult, op1=ALU.add),
                             reads=[B_ps[bk], B_gt[L], B_x[dc][tc]], writes=[B_x[dc][tc]])
                    RB.release(k)
                    if pend is not None:
                        pend()
                        pend = None
                    if with_stats:
                        pend = stats_chunk(dc, defer=True)
                if pend is not None:
                    pend()

            pcnt = {"n": 0}

            def mixer0(tb, stats_after):
                for g in range(NG):
                    for el in range(G):
                        ec = g * G + el
                        q = ec % 2
                        ys = ec % NY
                        kt = {}
                        for j in (1, 2, 0, 3):
                            kt[j] = RA.acquire(A_W0 + ec * 4 + j)
                        sl = {j: kt[j] % NA for j in kt}
                        S.op("scalar", lambda e, ec=ec, q=q: e.activation(
                            out=vbuf[:, q, 0:2], in_=cvst[:, ec, :], func=AF.Copy),
                             reads=[B_st[ec]], writes=[B_vbuf[q]])
                        for tc in range(NTC):
                            o = tc * TC
                            tq = tc
                            if tb == 0:
                                ada_piece()
                            pr = (pcnt["n"] % 3) * 2
                            pcnt["n"] += 1
                            for kd in range(KD):
                                for j, bk in ((1, pr), (2, pr + 1)):
                                    S.op("tensor", lambda e, s=sl[j], kd=kd, o=o, bk=bk: e.matmul(
                                        bank(bk), ringA[:, s, kd * 128:(kd + 1) * 128], h[:, kd, o:o + TC],
                                        start=(kd == 0), stop=(kd == KD - 1)),
                                         reads=[B_A[sl[j]], B_h[kd][tc]], writes=[B_ps[bk]])
                            if tc == NTC - 1:
                                RA.release(kt[1])
                                RA.release(kt[2])
                            S.op("scalar", lambda e, pr=pr, tq=tq: e.activation(
                                out=half(0, tq), in_=bank(pr + 1), func=AF.Copy),
                                 reads=[B_ps[pr + 1]], writes=[B_T[0 + tq]])
                            S.op("vector", lambda e, pr=pr, tq=tq, q=q, o=o: e.tensor_tensor(
                                out=vbuf[:, q, 2 + o:2 + o + TC], in0=bank(pr), in1=half(0, tq), op=ALU.mult),
                                 reads=[B_ps[pr], B_T[0 + tq]], writes=[B_vbuf[q]])
                            S.op("vector", lambda e, q=q, o=o, tq=tq, ec=ec: e.tensor_scalar(
                                out=half(1, tq), in0=vbuf[:, q, 2 + o:2 + o + TC],
                                scalar1=par[:, P_SCW + 64 + ec:P_SCW + 65 + ec], scalar2=None, op0=ALU.mult),
                                 reads=[B_vbuf[q], B_par], writes=[B_T[2 + tq]])
                            for kk, sh in ((1, 1), (0, 0)):
                                S.op("vector", lambda e, q=q, o=o, tq=tq, ec=ec, kk=kk, sh=sh: e.scalar_tensor_tensor(
                                    out=half(1, tq), in0=vbuf[:, q, sh + o:sh + o + TC],
                                    scalar=par[:, P_SCW + kk * 32 + ec:P_SCW + kk * 32 + ec + 1], in1=half(1, tq),
                                    op0=ALU.mult, op1=ALU.add),
                                     reads=[B_vbuf[q], B_par, B_T[2 + tq]], writes=[B_T[2 + tq]])
                            if tb == 0:
                                ada_piece()
                            pr2 = (pcnt["n"] % 3) * 2
                            pcnt["n"] += 1
                            for kd in range(KD):
                                for j, bk in ((0, pr2), (3, pr2 + 1)):
                                    S.op("tensor", lambda e, s=sl[j], kd=kd, o=o, bk=bk: e.matmul(
                                        bank(bk), ringA[:, s, kd * 128:(kd + 1) * 128], h[:, kd, o:o + TC],
                                        start=(kd == 0), stop=(kd == KD - 1)),
                                         reads=[B_A[sl[j]], B_h[kd][tc]], writes=[B_ps[bk]])
                            if tc == NTC - 1:
                                RA.release(kt[0])
                                RA.release(kt[3])
                            S.op("scalar", lambda e, pr2=pr2, tq=tq: e.activation(
                                out=half(2, tq), in_=bank(pr2 + 1), func=AF.Silu),
                                 reads=[B_ps[pr2 + 1]], writes=[B_T[4 + tq]])
                            S.op("vector", lambda e, pr2=pr2, tq=tq: e.tensor_tensor(
                                out=half(3, tq), in0=bank(pr2), in1=half(2, tq), op=ALU.mult),
                                 reads=[B_ps[pr2], B_T[4 + tq]], writes=[B_T[6 + tq]])
                            S.op("vector", lambda e, tq=tq, ys=ys, o=o: e.tensor_tensor(
                                out=ybuf[:, ys, o:o + TC], in0=half(3, tq), in1=half(1, tq), op=ALU.mult),
                                 reads=[B_T[6 + tq], B_T[2 + tq]], writes=[B_y[ys]])
                        S.op("scalar", lambda e, ec=ec, q=q: e.activation(
                            out=cvst[:, ec, :], in_=vbuf[:, q, NT:NT + 2], func=AF.Copy),
                             reads=[B_vbuf[q]], writes=[B_st[ec]])
                        if el == 1 and g > 0:
                            out_proj(0, g - 1, False)
                if tb == 0:
                    ada_piece(16 * 8)
                out_proj(0, NG - 1, stats_after)

            def mixer1(stats_after):
                def Vstep(ec):
                    q = ec % 2
                    vs = ec % 3
                    cs = ec % 4
                    k = RA.acquire(A_W1 + ec * 2 + 0)
                    s = k % NA
                    S.op("scalar", lambda e: e.activation(out=vbuf[:, q, 0:3], in_=vst[:, ec, :], func=AF.Copy),
                         reads=[B_st[ec]], writes=[B_vbuf[q]])
                    for tc in range(NTC):
                        for kd in range(KD):
                            S.op("tensor", lambda e, kd=kd, tc=tc: e.matmul(
                                bank(tc), ringA[:, s, kd * 128:(kd + 1) * 128], h[:, kd, tc * TC:(tc + 1) * TC],
                                start=(kd == 0), stop=(kd == KD - 1)),
                                 reads=[B_A[s], B_h[kd][tc]], writes=[B_ps[tc]])
                    RA.release(k)
                    S.op("scalar", lambda e: e.activation(out=vbuf[:, q, 3:3 + NT], in_=bank(0, 2), func=AF.Copy),
                         reads=[B_ps[0], B_ps[1]], writes=[B_vbuf[q]])
                    S.op("scalar", lambda e: e.activation(
                        out=big(vs), in_=vbuf[:, q, 3:3 + NT], func=AF.Identity,
                        scale=par[:, P_LCW + 96 + ec:P_LCW + 97 + ec], bias=par[:, P_LCB + ec:P_LCB + ec + 1]),
                         reads=[B_vbuf[q], B_par], writes=bb(vs))
                    for kk, eng in ((2, "vector"), (1, "vector"), (0, "vector")):
                        S.op(eng, lambda e, kk=kk: e.scalar_tensor_tensor(
                            out=big(vs), in0=vbuf[:, q, kk:kk + NT],
                            scalar=par[:, P_LCW + kk * 32 + ec:P_LCW + kk * 32 + ec + 1], in1=big(vs),
                            op0=ALU.mult, op1=ALU.add),
                             reads=[B_vbuf[q], B_par] + bb(vs), writes=bb(vs))
                    S.op("vector", lambda e: e.tensor_copy(out=vcb[:, cs, :], in_=big(vs)),
                         reads=bb(vs), writes=[B_vcb[cs]])
                    S.op("scalar", lambda e: e.activation(out=vst[:, ec, :], in_=vbuf[:, q, NT:NT + 3], func=AF.Copy),
                         reads=[B_vbuf[q]], writes=[B_st[ec]])

                def Gstep(ec, low_pair=False):
                    gq = ec % 2
                    pb = 2 if (gq == 0 or low_pair) else 6
                    k = RA.acquire(A_W1 + ec * 2 + 1)
                    s = k % NA
                    for tc in range(NTC):
                        for kd in range(KD):
                            S.op("tensor", lambda e, kd=kd, tc=tc: e.matmul(
                                bank(pb + tc), ringA[:, s, kd * 128:(kd + 1) * 128], h[:, kd, tc * TC:(tc + 1) * TC],
                                start=(kd == 0), stop=(kd == KD - 1)),
                                 reads=[B_A[s], B_h[kd][tc]], writes=[B_ps[pb + tc]])
                    RA.release(k)
                    S.op("scalar", lambda e: e.activation(out=big(3 + gq), in_=bank(pb, 2), func=AF.Tanh, scale=0.5),
                         reads=[B_ps[pb], B_ps[pb + 1]], writes=bb(3 + gq))
                    S.op("vector", lambda e: e.scalar_tensor_tensor(
                        out=big(3 + gq), in0=big(3 + gq), scalar=1.0, in1=bank(pb, 2), op0=ALU.add, op1=ALU.mult),
                         reads=bb(3 + gq) + [B_ps[pb], B_ps[pb + 1]], writes=bb(3 + gq))

                def gate_mm(hd, oc, ax, sG):
                    for tc in range(NTC):
                        for ic in range(2):
                            col = ((ax * 2 + ic) * 2 + oc) * 128
                            cs = (2 * hd + ic) % 4
                            S.op("tensor", lambda e, col=col, ic=ic, tc=tc, cs=cs: e.matmul(
                                bank(4 + tc), ringG[:, sG, col:col + 128], vcb[:, cs, tc * TC:(tc + 1) * TC],
                                start=(ic == 0), stop=(ic == 1)),
                                 reads=[B_G[sG], B_vcb[cs]], writes=[B_ps[4 + tc]])

                def Rstep(hd, oc, sG):
                    ec = 2 * hd + oc
                    rq = ec % 2
                    gate_mm(hd, oc, 0, sG)
                    S.op("scalar", lambda e: e.activation(out=big(5 + rq), in_=bank(4, 2), func=AF.Tanh, scale=0.5,
                                                          bias=der[:, D_HBA + ec:D_HBA + ec + 1]),
                         reads=[B_ps[4], B_ps[5], B_der], writes=bb(5 + rq))
                    S.op("scalar", lambda e: e.activation(out=big(8), in_=big(5 + rq), func=AF.Exp,
                                                          scale=der[:, D_CN + ec:D_CN + ec + 1],
                                                          bias=der[:, D_CN + ec:D_CN + ec + 1]),
                         reads=bb(5 + rq) + [B_der], writes=bb(8))
                    S.op("scalar", lambda e: e.activation(out=big(5 + rq), in_=big(5 + rq), func=AF.Exp,
                                                          scale=der[:, D_HC + ec:D_HC + ec + 1],
                                                          bias=der[:, D_HC + ec:D_HC + ec + 1]),
                         reads=bb(5 + rq) + [B_der], writes=bb(5 + rq))
                    S.op("scalar", lambda e: e.activation(out=big(8), in_=big(8), func=AF.Sqrt, scale=-0.0625,
                                                          bias=der[:, D_QTR:D_QTR + 1]),
                         reads=bb(8) + [B_der], writes=bb(8))

                def Istep(hd, oc, sG):
                    ec = 2 * hd + oc
                    rq = ec % 2
                    gq = ec % 2
                    vs = ec % 3
                    ys = ec % NY
                    gate_mm(hd, oc, 1, sG)
                    S.op("scalar", lambda e: e.activation(out=big(7), in_=bank(4, 2), func=AF.Tanh, scale=0.5,
                                                          bias=der[:, D_HBX + ec:D_HBX + ec + 1]),
                         reads=[B_ps[4], B_ps[5], B_der], writes=bb(7))
                    S.op("vector", lambda e: e.scalar_tensor_tensor(
                        out=big(7), in0=big(7), scalar=1.0, in1=big(vs), op0=ALU.add, op1=ALU.mult),
                         reads=bb(7) + bb(vs), writes=bb(7))
                    S.op("vector", lambda e: e.tensor_tensor(out=big(7), in0=big(8), in1=big(7), op=ALU.mult),
                         reads=bb(8) + bb(7), writes=bb(7))
                    S.op("vector", lambda e: e.tensor_tensor_scan(
                        out=big(9), data0=big(5 + rq), data1=big(7), initial=hst[:, ec:ec + 1],
                        op0=ALU.mult, op1=ALU.add),
                         reads=bb(5 + rq) + bb(7) + [B_hst[ec]], writes=bb(9))
                    S.op("gpsimd", lambda e: e.tensor_tensor(
                        out=ybuf[:, ys, :], in0=big(9), in1=big(3 + gq), op=ALU.mult),
                         reads=bb(9) + bb(3 + gq), writes=[B_y[ys]])
                    S.op("vector", lambda e: e.tensor_copy(
                        out=hst[:, ec:ec + 1], in_=tmp[:, 9 * 1024 + NT - 1:9 * 1024 + NT]),
                         reads=bb(9), writes=[B_hst[ec]])

                NH = KE // 2
                Vstep(0)
                Vstep(1)
                for hd in range(NH):
                    g = hd // (G // 2)
                    kG = RG.acquire(B_GATE + hd)
                    sG = kG % NGR
                    Gstep(2 * hd)
                    Rstep(hd, 0, sG)
                    if hd + 1 < NH:
                        Vstep(2 * hd + 2)
                    Istep(hd, 0, sG)
                    Gstep(2 * hd + 1, low_pair=(hd % (G // 2) == 0 and hd > 0))
                    Rstep(hd, 1, sG)
                    if hd % (G // 2) == 0 and hd > 0:
                        out_proj(1, g - 1, False)
                    if hd + 1 < NH:
                        Vstep(2 * hd + 3)
                    Istep(hd, 1, sG)
                    RG.release(kG)
                out_proj(1, NG - 1, stats_after)

            store_ops = []

            def load_x(tb, kd, stats=True):
                S.op("sync", lambda e, kd=kd, tb=tb: e.dma_start(
                    out=xres[:, kd, :], in_=xT[kd * 128:(kd + 1) * 128, tb * NT:(tb + 1) * NT]),
                     writes=B_x[kd], dma_sem=semX[kd])
                if stats:
                    stats_chunk(kd)

            def final_phase(tb, load_next):
                if final_norm:
                    rstd_ops()
                LAG = 3
                for kd in range(KD):
                    q = kd % 4
                    if final_norm:
                        S.op("vector", lambda e, kd=kd, q=q: e.scalar_tensor_tensor(
                            out=big(5 + q), in0=xres[:, kd, :], scalar=par[:, P_FG + kd:P_FG + kd + 1], in1=big(9),
                            op0=ALU.mult, op1=ALU.mult),
                             reads=B_x[kd] + bb(9) + [B_par], writes=bb(5 + q))
                    else:
                        S.op("scalar", lambda e, kd=kd, q=q: e.activation(
                            out=big(5 + q), in_=xres[:, kd, :], func=AF.Copy),
                             reads=B_x[kd], writes=bb(5 + q))
                    st = S.op("sync", lambda e, kd=kd, q=q, tb=tb: e.dma_start(
                        out=outT[kd * 128:(kd + 1) * 128, tb * NT:(tb + 1) * NT], in_=big(5 + q)),
                              reads=bb(5 + q), dma_sem=semO[q])
                    store_ops.append(st)
                    if load_next:
                        load_x(tb + 1, kd, stats=False)
                        if kd >= LAG:
                            stats_chunk(kd - LAG)
                if load_next:
                    for kd in range(KD - LAG, KD):
                        stats_chunk(kd)

            for kd in range(KD):
                load_x(0, kd)
            L0 = layers[0]
            for fb in range(8):
                for kq in range(4):
                    k = RA.acquire(A_ADA0 + fb * 4 + kq)
                    s_ = k % NA
                    for j in range(4):
                        ada_mm(ringA, s_, j, 4 * kq + j)
                    RA.release(k)
                ada_finish(L0, fb * 4)
            asc_op(L0)
            for tb in range(nblk):
                for li, L in enumerate(layers):
                    rstd_ops()
                    normalize(L)
                    last = (li == len(layers) - 1)
                    stats_after = (not last) or final_norm
                    if L == 0:
                        mixer0(tb, stats_after)
                    else:
                        if tb == 0:
                            ada_piece(16 * 8)
                        mixer1(stats_after)
                final_phase(tb, tb + 1 < nblk)
            S.op("sync", lambda e: e.nop(), extra=store_ops)

        RA0, RB0, RG0 = Ring(NA, None), Ring(NB, None), Ring(NGR, None)
        flow(Sched(dry=True), RA0, RB0, RG0)
        S = Sched()
        flow(S, Ring(NA, RA0.rec), Ring(NB, RB0.rec), Ring(NGR, RG0.rec))
        S.finalize(eng_sems)
        with nc.Block() as block:
            @block.tensor
            def _(e):
                S.emit("tensor", e)

            @block.scalar
            def _(e):
                S.emit("scalar", e)

            @block.vector
            def _(e):
                S.emit("vector", e)

            @block.gpsimd
            def _(e):
                S.emit("gpsimd", e)

            @block.sync
            def _(e):
                S.emit("sync", e)
    return nc


def _tilesA(W):
    F = W.shape[1]
    return np.ascontiguousarray(W.reshape(16, 128, F // 128, 128).transpose(2, 1, 0, 3)).reshape(F // 128, 128, 2048)


def _cols(v):
    v = np.asarray(v, np.float32).reshape(-1, 128)
    return v.T


def prep_shared(inp, first_layer=0):
    tA = np.empty((NTILE_A, 128, 2048), np.float32)
    We = inp["ada_w"][first_layer][:, :4096]
    tA[A_ADA0:A_ADA0 + 32] = np.ascontiguousarray(
        We.reshape(4, 4, 128, 8, 512).transpose(3, 0, 2, 1, 4)).reshape(32, 128, 2048)
    w0 = _tilesA(inp["sc_w_in"][0])
    tA[A_W0:A_W0 + 128] = w0.reshape(4, 32, 128, 2048).transpose(1, 0, 2, 3).reshape(128, 128, 2048)
    w1 = _tilesA(inp["lru_w_in"][0])
    tA[A_W1:A_W1 + 64] = w1.reshape(2, 32, 128, 2048).transpose(1, 0, 2, 3).reshape(64, 128, 2048)

    tB = np.empty((NTILE_B, 128, 1024), np.float32)

    def wout(W):
        return np.ascontiguousarray(W.reshape(NG, G, 128, KD, 128).transpose(0, 3, 2, 1, 4)).reshape(NG * KD, 128, 1024)

    tB[B_OUT0:B_OUT0 + 64] = wout(inp["sc_w_out"][0])
    tB[B_OUT1:B_OUT1 + 64] = wout(inp["lru_w_out"][0])
    wa = inp["lru_w_a"][0].reshape(16, 2, 128, 2, 128)
    wx = inp["lru_w_x"][0].reshape(16, 2, 128, 2, 128)
    gt = np.stack([wa, wx], axis=0)
    tB[B_GATE:B_GATE + 16] = np.ascontiguousarray(gt.transpose(1, 3, 0, 2, 4, 5)).reshape(16, 128, 1024)
    Wl = np.concatenate([inp["ada_w"][0][:, 4096:], inp["ada_w"][1]], axis=1)
    tB[B_ADAL:B_ADAL + 128] = np.ascontiguousarray(
        Wl.reshape(8, 2, 128, 16, 512).transpose(3, 0, 2, 1, 4)).reshape(128, 128, 1024)

    par = np.zeros((128, NPAR), np.float32)
    par[:, P_G0:P_G0 + 16] = _cols(inp["norm_g"][0])
    par[:, P_G1:P_G1 + 16] = _cols(inp["norm_g"][1])
    par[:, P_FG:P_FG + 16] = _cols(inp["final_g"])
    par[:, P_AB0:P_AB0 + 48] = _cols(inp["ada_b"][0])
    par[:, P_AB1:P_AB1 + 48] = _cols(inp["ada_b"][1])
    for k in range(3):
        par[:, P_SCW + k * 32:P_SCW + (k + 1) * 32] = _cols(inp["sc_conv_w"][0][k])
    for k in range(4):
        par[:, P_LCW + k * 32:P_LCW + (k + 1) * 32] = _cols(inp["lru_conv_w"][0][k])
    par[:, P_LCB:P_LCB + 32] = _cols(inp["lru_conv_b"][0])
    par[:, P_BA:P_BA + 32] = _cols(inp["lru_b_a"][0])
    par[:, P_BX:P_BX + 32] = _cols(inp["lru_b_x"][0])
    par[:, P_LAM:P_LAM + 32] = _cols(inp["lru_lambda"][0])
    return tA, tB, par


def kernel(**inputs):
    inp = {k: np.asarray(v) for k, v in inputs.items()}
    tA, tB, par = prep_shared(inp)
    x = inp["x"]
    c = inp["c"]
    nb = x.shape[0]
    in_maps = []
    for b in range(nb):
        p = par.copy()
        p[:, P_C:P_C + 16] = _cols(c[b])
        in_maps.append({"xT": np.ascontiguousarray(x[b].T), "tA": tA, "tB": tB, "par": p})
    nc = build_program()
    res = run_bass_kernel_spmd(nc, in_maps, core_ids=list(range(nb)))
    out = np.empty_like(x)
    for b in range(nb):
        out[b] = res.results[b]["outT"].T
    return out
```
